# Optimizing a Trainium2 kernel written in Bass

```python
import math
import jax, jax.numpy as jnp
from jax import lax
import numpy as np

D_MODEL = 1024
BATCH = 4
SEQ = 8192
DEPTH = 1

GRID_W = 64
N_HEADS = 8
HEAD_DIM = 64
ATTN_W = N_HEADS * HEAD_DIM
NA_WIN_H = 8
NA_WIN_W = 16
SSM_W = 512
SSM_GROUP = 16
SSM_GROUPS = SSM_W // SSM_GROUP
SSM_STATE = 64
N_DIR = 2
DT_MIN = 1e-3
DT_MAX = 1e-1
LAMBDA_RE_MAX = -1e-4
N_BRANCH = 2
IN_COLS = 3 * ATTN_W + SSM_W + N_BRANCH * D_MODEL
D_FF = 2816
PLE_DIM = 256
RMS_EPS = 1e-6

kernel_name = "hybrid_natten_s5_macaron_block"


def rms_norm(x, g):
    xf = x.astype(jnp.float32)
    y = xf * lax.rsqrt(jnp.mean(xf * xf, axis=-1, keepdims=True) + RMS_EPS)
    return (y * g.astype(jnp.float32)).astype(x.dtype)


def swiglu(x, w_gate, w_up, w_down):
    return (jax.nn.silu(x @ w_gate) * (x @ w_up)) @ w_down


def neighbourhood_attention(q, k, v, rpb):
    bn, s, h, hd = q.shape
    rows = s // GRID_W
    wh = min(NA_WIN_H, rows)
    ww = NA_WIN_W
    scale = hd ** -0.5
    qg = q.reshape(bn, rows, GRID_W, h, hd)
    kg = k.reshape(bn, rows, GRID_W, h, hd)
    vg = v.reshape(bn, rows, GRID_W, h, hd)
    cols = jnp.arange(GRID_W)
    col_start = jnp.clip(cols - ww // 2, 0, GRID_W - ww)
    col_idx = col_start[:, None] + jnp.arange(ww)[None, :]
    dc_idx = col_idx - cols[:, None] + (NA_WIN_W - 1)

    def row_block(r):
        rs = jnp.clip(r - wh // 2, 0, rows - wh)
        kr = lax.dynamic_slice_in_dim(kg, rs, wh, axis=1)
        vr = lax.dynamic_slice_in_dim(vg, rs, wh, axis=1)
        kn = kr[:, :, col_idx]
        vn = vr[:, :, col_idx]
        qr = lax.dynamic_index_in_dim(qg, r, axis=1, keepdims=False)
        dr_idx = rs + jnp.arange(wh) - r + (NA_WIN_H - 1)
        bias = rpb[:, dr_idx[:, None, None], dc_idx[None]]
        bias = bias.transpose(0, 2, 1, 3)
        sc = jnp.einsum('bchd,brcwhd->bhcrw', qr, kn) * scale + bias[None]
        sc = sc.reshape(bn, h, GRID_W, wh * ww).astype(jnp.float32)
        pr = jax.nn.softmax(sc, axis=-1).astype(v.dtype).reshape(bn, h, GRID_W, wh, ww)
        return jnp.einsum('bhcrw,brcwhd->bchd', pr, vn)

    out = lax.map(row_block, jnp.arange(rows))
    return out.transpose(1, 0, 2, 3, 4).reshape(bn, s, h * hd)


def _ssm_combine(ei, ej):
    a_i, b_i = ei
    a_j, b_j = ej
    return a_j * a_i, a_j * b_i + b_j


def s5_scan(u, lam_re, lam_im, log_dt, b_re, b_im, c_re, c_im, reverse):
    f32 = jnp.float32
    s = u.shape[1]
    lam = lax.complex(jnp.minimum(lam_re.astype(f32), LAMBDA_RE_MAX), lam_im.astype(f32))
    dt = jnp.exp(log_dt.astype(f32))[:, None]
    lam_bar = jnp.exp(lam * dt)
    b_bar = ((lam_bar - 1.0) / lam)[..., None] * lax.complex(b_re.astype(f32), b_im.astype(f32))
    bu = lax.complex(jnp.einsum('bsgi,gpi->bsgp', u, b_bar.real),
                     jnp.einsum('bsgi,gpi->bsgp', u, b_bar.imag))
    a = jnp.broadcast_to(lam_bar, (1, s) + lam_bar.shape)
    _, states = lax.associative_scan(_ssm_combine, (a, bu), reverse=reverse, axis=1)
    return (jnp.einsum('bsgp,gop->bsgo', states.real, c_re.astype(f32))
            - jnp.einsum('bsgp,gop->bsgo', states.imag, c_im.astype(f32)))


def s5_branch(s_in, lam_re, lam_im, log_dt, b_re, b_im, c_re, c_im, d_skip, glu_w, glu_b):
    f32 = jnp.float32
    bn, s, _ = s_in.shape
    u = s_in.astype(f32).reshape(bn, s, SSM_GROUPS, SSM_GROUP)
    y = d_skip.astype(f32).reshape(SSM_GROUPS, SSM_GROUP) * u
    for direction in range(N_DIR):
        y = y + s5_scan(u, lam_re[direction], lam_im[direction], log_dt[direction],
                        b_re[direction], b_im[direction], c_re[direction], c_im[direction],
                        reverse=(direction == 1))
    y = jax.nn.gelu(y.reshape(bn, s, SSM_W))
    y = y * jax.nn.sigmoid(y @ glu_w.astype(f32) + glu_b.astype(f32))
    return y.astype(s_in.dtype)


def setup_inputs(seed: int = 0) -> dict:
    key = jax.random.key(seed)
    ks = iter(jax.random.split(key, 48))
    L = DEPTH
    f32 = jnp.float32

    def nrm(shape, fan_in):
        return jax.random.normal(next(ks), shape, f32) * fan_in ** -0.5

    def gain(shape):
        return 1.0 + 0.05 * jax.random.normal(next(ks), shape, f32)

    def small(shape, scale):
        return scale * jax.random.normal(next(ks), shape, f32)

    G, P, GC = SSM_GROUPS, SSM_STATE, SSM_GROUP
    return {
        "x": jax.random.normal(next(ks), (BATCH, SEQ, D_MODEL), f32),
        "p": jax.random.normal(next(ks), (DEPTH, BATCH, SEQ, PLE_DIM), f32),
        "ffn1_norm": gain((L, D_MODEL)),
        "ffn1_w_gate": nrm((L, D_MODEL, D_FF), D_MODEL),
        "ffn1_w_up": nrm((L, D_MODEL, D_FF), D_MODEL),
        "ffn1_w_down": nrm((L, D_FF, D_MODEL), D_FF),
        "mix_norm": gain((L, D_MODEL)),
        "w_in": nrm((L, D_MODEL, IN_COLS), D_MODEL),
        "na_rpb": small((L, N_HEADS, 2 * NA_WIN_H - 1, 2 * NA_WIN_W - 1), 0.02),
        "ssm_lam_re": -0.5 + small((L, N_DIR, G, P), 0.01),
        "ssm_lam_im": math.pi * jnp.arange(P, dtype=f32) + small((L, N_DIR, G, P), 0.01),
        "ssm_log_dt": jax.random.uniform(next(ks), (L, N_DIR, G), f32,
                                         minval=math.log(DT_MIN), maxval=math.log(DT_MAX)),
        "ssm_b_re": nrm((L, N_DIR, G, P, GC), 2 * GC),
        "ssm_b_im": nrm((L, N_DIR, G, P, GC), 2 * GC),
        "ssm_c_re": nrm((L, N_DIR, G, GC, P), 2 * P),
        "ssm_c_im": nrm((L, N_DIR, G, GC, P), 2 * P),
        "ssm_d": jax.random.normal(next(ks), (L, SSM_W), f32),
        "ssm_glu_w": nrm((L, SSM_W, SSM_W), SSM_W),
        "ssm_glu_b": small((L, SSM_W), 0.01),
        "w_attn_out": nrm((L, ATTN_W, D_MODEL), ATTN_W),
        "w_ssm_out": nrm((L, SSM_W, D_MODEL), SSM_W),
        "w_out": nrm((L, D_MODEL, D_MODEL), D_MODEL),
        "ffn2_norm": gain((L, D_MODEL)),
        "ffn2_w_gate": nrm((L, D_MODEL, D_FF), D_MODEL),
        "ffn2_w_up": nrm((L, D_MODEL, D_FF), D_MODEL),
        "ffn2_w_down": nrm((L, D_FF, D_MODEL), D_FF),
        "ple_norm": gain((L, D_MODEL)),
        "ple_w_gate": nrm((L, D_MODEL, D_MODEL), D_MODEL),
        "ple_w_proj": nrm((L, PLE_DIM, D_MODEL), PLE_DIM),
        "final_norm": gain((D_MODEL,)),
    }


def reference(x, p, ffn1_norm, ffn1_w_gate, ffn1_w_up, ffn1_w_down, mix_norm, w_in, na_rpb,
              ssm_lam_re, ssm_lam_im, ssm_log_dt, ssm_b_re, ssm_b_im, ssm_c_re, ssm_c_im,
              ssm_d, ssm_glu_w, ssm_glu_b, w_attn_out, w_ssm_out, w_out,
              ffn2_norm, ffn2_w_gate, ffn2_w_up, ffn2_w_down,
              ple_norm, ple_w_gate, ple_w_proj, final_norm):
    bn, s, _ = x.shape
    splits = [ATTN_W, 2 * ATTN_W, 3 * ATTN_W, 3 * ATTN_W + SSM_W]
    h = x
    for i in range(DEPTH):
        h = h + 0.5 * swiglu(rms_norm(h, ffn1_norm[i]), ffn1_w_gate[i], ffn1_w_up[i], ffn1_w_down[i])
        u = rms_norm(h, mix_norm[i])
        z = u @ w_in[i]
        q, k, v, s_in, gates = jnp.split(z, splits, axis=-1)
        y_attn = neighbourhood_attention(q.reshape(bn, s, N_HEADS, HEAD_DIM),
                                         k.reshape(bn, s, N_HEADS, HEAD_DIM),
                                         v.reshape(bn, s, N_HEADS, HEAD_DIM),
                                         na_rpb[i]) @ w_attn_out[i]
        y_ssm = s5_branch(s_in, ssm_lam_re[i], ssm_lam_im[i], ssm_log_dt[i],
                          ssm_b_re[i], ssm_b_im[i], ssm_c_re[i], ssm_c_im[i],
                          ssm_d[i], ssm_glu_w[i], ssm_glu_b[i]) @ w_ssm_out[i]
        g_attn, g_ssm = jnp.split(jax.nn.sigmoid(gates), N_BRANCH, axis=-1)
        h = h + (g_attn * y_attn + g_ssm * y_ssm) @ w_out[i]
        h = h + 0.5 * swiglu(rms_norm(h, ffn2_norm[i]), ffn2_w_gate[i], ffn2_w_up[i], ffn2_w_down[i])
        h = h + (p[i] @ ple_w_proj[i]) * jax.nn.sigmoid(rms_norm(h, ple_norm[i]) @ ple_w_gate[i])
    return rms_norm(h, final_norm)
```

```python
import numpy as np
import concourse.bass as bass
import concourse.mybir as mybir
from concourse.bass_utils import run_bass_kernel_spmd

F32 = mybir.dt.float32
BF16 = mybir.dt.bfloat16
AF = mybir.ActivationFunctionType
ALU = mybir.AluOpType

D = 1024
DFF = 2816
NLOC = 4608
NOWN = 4096
HALO = 256
NCORES = 8
RMS_EPS = 1e-6
MASKV = -30000.0

STUB_ATTN = False
STUB_SSM = False
DEBUG = False
DBG = {}


class R:
    __slots__ = ("name", "lw", "rd")

    def __init__(self, name=""):
        self.name = name
        self.lw = None
        self.rd = []


class Sched:
    def __init__(self, nc, n_dma_sems=48):
        self.nc = nc
        self.eng = {"pe": nc.tensor, "act": nc.scalar, "dve": nc.vector, "pool": nc.gpsimd, "sp": nc.sync}
        self.sem = {k: nc.alloc_semaphore("sem_" + k) for k in self.eng}
        self.cnt = {k: 0 for k in self.eng}
        self.known = {k: {} for k in self.eng}
        self.dsems = [nc.alloc_semaphore("dsem%d" % i) for i in range(n_dma_sems)]
        self.dcnt = [0] * n_dma_sems
        half = n_dma_sems // 2
        self.dq = {"sp": list(range(0, half)), "pool": list(range(half, n_dma_sems))}
        self.dqn = {"sp": 0, "pool": 0}
        self.prog = {k: [] for k in self.eng}
        self.banks = []
        self.bnext = 0
        self.ccsem = nc.alloc_semaphore("ccsem")
        self.cccnt = 0

    def _wait(self, e, ev):
        if ev is None:
            return
        key, semh, val = ev
        if self.known[e].get(key, 0) >= val:
            return
        if key == "E" + e and val > self.cnt[e]:
            return
        self.prog[e].append(lambda E, semh=semh, val=val: E.wait_ge(semh, val))
        self.known[e][key] = val

    def _deps(self, e, reads, writes):
        best = {}
        for r in reads:
            ev = r.lw
            if ev is not None and (ev[0] not in best or best[ev[0]][2] < ev[2]):
                best[ev[0]] = ev
        for w in writes:
            ev = w.lw
            if ev is not None and (ev[0] not in best or best[ev[0]][2] < ev[2]):
                best[ev[0]] = ev
            for ev in w.rd:
                if ev[0] not in best or best[ev[0]][2] < ev[2]:
                    best[ev[0]] = ev
        for ev in best.values():
            self._wait(e, ev)

    def _commit(self, ev, reads, writes):
        for r in reads:
            r.rd.append(ev)
            if len(r.rd) > 16:
                best = {}
                for x in r.rd:
                    if x[0] not in best or best[x[0]][2] < x[2]:
                        best[x[0]] = x
                r.rd = list(best.values())
        for w in writes:
            w.lw = ev
            w.rd = []

    def op(self, e, fn, reads=(), writes=(), track=True):
        self._deps(e, reads, writes)
        if track:
            self.cnt[e] += 1
            sem = self.sem[e]
            self.prog[e].append(lambda E, fn=fn, sem=sem: fn(E).then_inc(sem, 1))
            ev = ("E" + e, self.sem[e], self.cnt[e])
        else:
            self.prog[e].append(lambda E, fn=fn: fn(E))
            ev = ("E" + e, self.sem[e], self.cnt[e] + 1)
        self._commit(ev, reads, writes)
        return ev

    def dma(self, q, out, in_, reads=(), writes=(), **kw):
        self._deps(q, reads, writes)
        lst = self.dq[q]
        j = lst[self.dqn[q] % len(lst)]
        self.dqn[q] += 1
        if self.dcnt[j] > 0:
            self._wait(q, ("D%d" % j, self.dsems[j], 16 * self.dcnt[j]))
        self.dcnt[j] += 1
        dsem = self.dsems[j]
        self.prog[q].append(
            lambda E, out=out, in_=in_, kw=kw, dsem=dsem: E.dma_start(out=out, in_=in_, **kw).then_inc(dsem, 16))
        ev = ("D%d" % j, self.dsems[j], 16 * self.dcnt[j])
        self._commit(ev, reads, writes)
        return ev

    def bank(self):
        b = self.banks[self.bnext]
        self.bnext = (self.bnext + 1) % len(self.banks)
        return b

    def barrier(self, token_dram):
        for j in range(len(self.dsems)):
            if self.dcnt[j]:
                self._wait("sp", ("D%d" % j, self.dsems[j], 16 * self.dcnt[j]))
        for k in self.eng:
            if k != "sp" and self.cnt[k]:
                self._wait("sp", ("E" + k, self.sem[k], self.cnt[k]))
        r = R()
        ev = self.dma("sp", token_dram[0:1, 0:8], token_dram[1:2, 0:8], writes=[r])
        for k in self.eng:
            self._wait(k, ev)

    def emit(self):
        for j in range(len(self.dsems)):
            if self.dcnt[j]:
                self._wait("sp", ("D%d" % j, self.dsems[j], 16 * self.dcnt[j]))
        for k in self.eng:
            if k != "sp" and self.cnt[k]:
                self._wait("sp", ("E" + k, self.sem[k], self.cnt[k]))
        names = {"pe": "tensor", "act": "scalar", "dve": "vector", "pool": "gpsimd", "sp": "sync"}
        with self.nc.Block() as block:
            for k, prog in self.prog.items():
                if not prog:
                    continue

                def body(E, prog=prog):
                    for t in prog:
                        t(E)
                getattr(block, names[k])(body)


class Arena:
    def __init__(self, nc, nbytes):
        self.t = nc.alloc_sbuf_tensor("arena", [128, nbytes // 2], BF16)
        self.nbytes = nbytes
        self.off = 0

    def take(self, shape, dtype, parts=128):
        esz = 4 if dtype == F32 else 2
        n = 1
        for s in shape:
            n *= s
        nb = n * esz
        o = (self.off + 63) // 64 * 64
        assert o + nb <= self.nbytes, ("arena overflow", o, nb, self.nbytes)
        self.off = o + nb
        v = self.t[0:parts, o // 2:(o + nb) // 2]
        if dtype == F32:
            v = v.bitcast(F32)
        if len(shape) == 2:
            v = v.rearrange("p (a b) -> p a b", b=shape[1])
        elif len(shape) == 3:
            v = v.rearrange("p (a b c) -> p a b c", b=shape[1], c=shape[2])
        elif len(shape) == 4:
            v = v.rearrange("p (a b c d) -> p a b c d", b=shape[1], c=shape[2], d=shape[3])
        return v


class Builder:
    def __init__(self):
        nc = bass.Bass("TRN2", target_bir_lowering=False)
        self.nc = nc
        self.S = Sched(nc)
        S = self.S
        dt = nc.dram_tensor
        self.xT = dt("xT", [D, NLOC], F32, kind="ExternalInput").ap()
        self.pT = dt("pT", [256, NOWN], F32, kind="ExternalInput").ap()
        self.w_gu1 = dt("w_gu1", [11, 128, 4096], F32, kind="ExternalInput").ap()
        self.w_d1 = dt("w_d1", [8, 128, 2816], F32, kind="ExternalInput").ap()
        self.w_gu2 = dt("w_gu2", [11, 128, 4096], F32, kind="ExternalInput").ap()
        self.w_d2 = dt("w_d2", [8, 128, 2816], F32, kind="ExternalInput").ap()
        self.w_in = dt("w_in", [8, 128, 4096], F32, kind="ExternalInput").ap()
        self.w_out = dt("w_out", [2, 128, 4096], F32, kind="ExternalInput").ap()
        self.w_pg = dt("w_pg", [2, 128, 4096], F32, kind="ExternalInput").ap()
        self.w_ao = dt("w_ao", [128, 4096], F32, kind="ExternalInput").ap()
        self.w_so = dt("w_so", [128, 4096], F32, kind="ExternalInput").ap()
        self.w_pp = dt("w_pp", [128, 2048], F32, kind="ExternalInput").ap()
        self.w_glu = dt("w_glu", [128, 2048], F32, kind="ExternalInput").ap()
        self.gains = dt("gains", [128, 48], F32, kind="ExternalInput").ap()
        self.glu_b = dt("glu_b", [128, 4], F32, kind="ExternalInput").ap()
        self.cst_ident = dt("cst_ident", [128, 128], F32, kind="ExternalInput").ap()
        self.rpb = dt("rpb", [8, 15, 31], F32, kind="ExternalInput").ap()
        self.maskcols = dt("maskcols", [128, 45], F32, kind="ExternalInput").ap()
        self.ssm_p = dt("ssm_p", [64, 3, 64], F32, kind="ExternalInput").ap()
        self.ssm_b = dt("ssm_b", [64, 2, 64, 16], F32, kind="ExternalInput").ap()
        self.ssm_c = dt("ssm_c", [64, 2, 64, 16], F32, kind="ExternalInput").ap()
        self.ssm_dd = dt("ssm_dd", [1, 512], F32, kind="ExternalInput").ap()
        self.cst_expo = dt("cst_expo", [64, 65], F32, kind="ExternalInput").ap()
        self.cst_eye16 = dt("cst_eye16", [1, 256], F32, kind="ExternalInput").ap()
        self.sel = dt("sel", [64, 16], F32, kind="ExternalInput").ap()
        self.ktab_s = dt("ktab_s", [32, 16, 127, 16], F32).ap()
        self.cc_src = dt("cc_src", [64, 128], F32)
        self.cc_dst = dt("cc_dst", [NCORES * 64, 128], F32)
        self.outT = dt("outT", [D, NOWN], F32, kind="ExternalOutput").ap()
        if DBG.get("dump"):
            self.dbg_attn = dt("dbg_attn", [512, NOWN], BF16, kind="ExternalOutput").ap()
            self.dbg_ssm = dt("dbg_ssm", [512, NOWN], BF16, kind="ExternalOutput").ap()
            self.dbg_E = dt("dbg_E", [64, 8320], F32, kind="ExternalOutput").ap()
            self.dbg_Bb = dt("dbg_Bb", [64, 2048], F32, kind="ExternalOutput").ap()
            self.dbg_kt = dt("dbg_kt", [16, 127 * 16], F32, kind="ExternalOutput").ap()
            self.dbg_G = dt("dbg_G", [64, 8192], F32, kind="ExternalOutput").ap()
            self.dbg_XP = dt("dbg_XP", [64, 8192], F32, kind="ExternalOutput").ap()
            self.dbg_Y = dt("dbg_Y", [64, 8192], BF16, kind="ExternalOutput").ap()
            self.dbg_U = dt("dbg_U", [128, 512], BF16, kind="ExternalOutput").ap()
        self.h1_s = dt("h1_s", [D, NOWN], F32).ap()
        self.qT_s = dt("qT_s", [512, NOWN], BF16).ap()
        self.kT_s = dt("kT_s", [512, NLOC], BF16).ap()
        self.v_s = dt("v_s", [NLOC, 512], BF16).ap()
        self.attnT_s = dt("attnT_s", [512, NOWN], BF16).ap()
        self.yssmT_s = dt("yssmT_s", [512, NOWN], BF16).ap()
        self.tok_s = dt("tok_s", [2, 8], F32).ap()
        for i in range(6):
            t = nc.alloc_psum_tensor("ps%d" % i, [128, 512], F32)
            S.banks.append((t[:], R("bank%d" % i)))
        self.psT = []
        for i in range(2):
            tb = nc.alloc_psum_tensor("psT%d" % i, [128, 1024], BF16)
            self.psT.append((tb[:, 0:512], R("psT%d" % i)))
        self.psT_n = 0
        self.A = Arena(nc, 206 * 1024)
        A = self.A
        self.ones_bf = A.take([128], BF16)
        self.ident_bf = A.take([128], BF16)
        self.ident_f = A.take([128], F32)
        self.gains_sb = A.take([48], F32)
        self.glub_sb = A.take([4], F32)
        self.epsc = A.take([1], F32)
        self.rC = R("consts")
        self.base_noU = A.off
        self.U_all = A.take([32, 8, 64], BF16)
        self.rU = R("U_all")
        self.base_off = A.off
        self.wcnt = 0

    def consts(self):
        S, nc = self.S, self.nc
        S.op("dve", lambda e: e.memset(self.ones_bf, 1.0), writes=[self.rC])
        S.op("dve", lambda e: e.memset(self.epsc, RMS_EPS), writes=[self.rC])
        S.dma("sp", self.ident_f, self.cst_ident[:, :], writes=[self.rC])
        S.dma("sp", self.gains_sb, self.gains[:, :], writes=[self.rC])
        S.dma("sp", self.glub_sb, self.glu_b[:, :], writes=[self.rC])
        S.op("dve", lambda e: e.tensor_copy(out=self.ident_bf, in_=self.ident_f), reads=[self.rC], writes=[self.rC])

    def gain(self, idx, kc):
        return self.gains_sb[:, idx * 8 + kc: idx * 8 + kc + 1]

    def wslot(self):
        s = self.wslots[self.wcnt % len(self.wslots)]
        self.wcnt += 1
        return s

    def load_w(self, dram_blk, KC, bw):
        t, r = self.wslot()
        v = t[:, 0:KC * bw]
        self.S.dma("pool", v, dram_blk, writes=[r])
        return v.rearrange("p (k w) -> p k w", w=bw), r

    def rmsnorm(self, src, rsrc, gidx, dst, rdst, subs):
        S = self.S
        for (t0, n) in subs:
            sq, rsq = self.sq, self.rsq
            S.op("act", lambda e, t0=t0, n=n: e.activation(out=sq[:, :, 0:n], in_=src[:, :, t0:t0 + n], func=AF.Square),
                 reads=[rsrc], writes=[rsq])
            ps, pr = S.bank()
            for kc in range(8):
                S.op("pe", lambda e, kc=kc, n=n, ps=ps: e.matmul(ps[:, 0:n], self.ones_bf, sq[:, kc, 0:n],
                                                               start=(kc == 0), stop=(kc == 7)),
                     reads=[rsq, self.rC], writes=[pr], track=(kc == 7))
            st, rst = self.rstd, self.rrstd
            S.op("act", lambda e, n=n, ps=ps: e.activation(out=st[:, 0:n], in_=ps[:, 0:n], func=AF.Ln,
                                                          bias=self.epsc, scale=1.0 / D),
                 reads=[pr, self.rC], writes=[rst])
            S.op("act", lambda e, n=n: e.activation(out=st[:, 0:n], in_=st[:, 0:n], func=AF.Exp, scale=-0.5),
                 reads=[rst], writes=[rst])
            for kc in range(8):
                S.op("dve", lambda e, kc=kc, t0=t0, n=n: e.scalar_tensor_tensor(
                    out=dst[:, kc, t0:t0 + n], in0=src[:, kc, t0:t0 + n], scalar=self.gain(gidx, kc),
                    in1=st[:, 0:n], op0=ALU.mult, op1=ALU.mult),
                     reads=[rsrc, rst, self.rC], writes=[rdst])

    def ffn(self, w_gu, w_d, xn, rxn, h, rh, subs):
        S = self.S
        aT, raT = self.aT, self.raT
        for mb in range(11):
            w, rw = self.load_w(w_gu[mb], 8, 512)
            for (t0, n) in subs:
                for cj in range(2):
                    hc = mb * 2 + cj
                    pg, rpg = S.bank()
                    pu, rpu = S.bank()
                    for kc in range(8):
                        S.op("pe", lambda e, kc=kc, cj=cj, t0=t0, n=n, pg=pg, w=w: e.matmul(
                            pg[:, 0:n], w[:, kc, cj * 128:(cj + 1) * 128], xn[:, kc, t0:t0 + n],
                            start=(kc == 0), stop=(kc == 7)), reads=[rw, rxn], writes=[rpg], track=(kc == 7))
                    for kc in range(8):
                        S.op("pe", lambda e, kc=kc, cj=cj, t0=t0, n=n, pu=pu, w=w: e.matmul(
                            pu[:, 0:n], w[:, kc, 256 + cj * 128:256 + (cj + 1) * 128], xn[:, kc, t0:t0 + n],
                            start=(kc == 0), stop=(kc == 7)), reads=[rw, rxn], writes=[rpu], track=(kc == 7))
                    tmp, rtmp = self.tmpf[self.tmpn % 2]
                    self.tmpn += 1
                    S.op("act", lambda e, n=n, pg=pg, tmp=tmp: e.activation(out=tmp[:, 0:n], in_=pg[:, 0:n], func=AF.Silu),
                         reads=[rpg], writes=[rtmp])
                    S.op("dve", lambda e, n=n, t0=t0, hc=hc, pu=pu, tmp=tmp: e.tensor_tensor(
                        out=aT[:, hc, t0:t0 + n], in0=tmp[:, 0:n], in1=pu[:, 0:n], op=ALU.mult),
                         reads=[rtmp, rpu], writes=[raT])
        for oc in range(8):
            w, rw = self.load_w(w_d[oc], 22, 128)
            for (t0, n) in subs:
                ps, pr = S.bank()
                for hc in range(22):
                    S.op("pe", lambda e, hc=hc, t0=t0, n=n, ps=ps, w=w: e.matmul(
                        ps[:, 0:n], w[:, hc, :], aT[:, hc, t0:t0 + n], start=(hc == 0), stop=(hc == 21)),
                         reads=[rw, raT], writes=[pr], track=(hc == 21))
                S.op("dve", lambda e, oc=oc, t0=t0, n=n, ps=ps: e.scalar_tensor_tensor(
                    out=h[:, oc, t0:t0 + n], in0=ps[:, 0:n], scalar=0.5, in1=h[:, oc, t0:t0 + n],
                    op0=ALU.mult, op1=ALU.add), reads=[pr, rh], writes=[rh])

    def alloc_tile_bufs(self, base=None):
        A = self.A
        A.off = self.base_off if base is None else base
        self.h = A.take([8, 1024], F32)
        self.rh = R("h")
        self.xn = A.take([8, 1024], BF16)
        self.rxn = R("xn")
        self.big_off = A.off
        self.aT = A.take([22, 1024], BF16)
        self.raT = R("aT")
        self.sq = A.take([8, 512], BF16)
        self.rsq = R("sq")
        self.rstd = A.take([512], F32)
        self.rrstd = R("rstd")
        self.tmpf = [(A.take([512], F32), R("tmp0")), (A.take([512], F32), R("tmp1"))]
        self.tmpn = 0
        self.wslots = [(A.take([4096], BF16), R("w%d" % i)) for i in range(4)]
        self.tile_end = A.off

    def phase1(self):
        S = self.S
        self.alloc_tile_bufs()
        A = self.A
        h, rh, xn, rxn = self.h, self.rh, self.xn, self.rxn
        xTv = self.xT.rearrange("(kc p) t -> p kc t", p=128)
        h1v = self.h1_s.rearrange("(kc p) t -> p kc t", p=128)
        qv = self.qT_s.rearrange("(c p) t -> p c t", p=128)
        kv = self.kT_s.rearrange("(c p) t -> p c t", p=128)
        save = A.off
        A.off = self.big_off
        qst = A.take([4, 1024], BF16)
        kst = A.take([4, 1024], BF16)
        vst = A.take([8, 512], BF16)
        zt = A.take([8, 512], BF16)
        A.off = save
        rst_ = self.raT
        tiles = [("own", i) for i in range(4)] + [("halo", 0)]
        tiles = DBG.get("tiles", tiles)
        for kind, ti in tiles:
            if kind == "own":
                segs = [(HALO + 1024 * ti, 0, 1024)]
                subs = [(0, 512), (512, 512)]
            else:
                segs = [(0, 0, 256), (NLOC - 256, 256, 256)]
                subs = [(0, 256), (256, 256)]
            for (lt, off, n) in segs:
                S.dma("sp", h[:, :, off:off + n], xTv[:, :, lt:lt + n], writes=[rh])
            self.rmsnorm(h, rh, 0, xn, rxn, subs)
            if not DBG.get("skip_ffn"):
                self.ffn(self.w_gu1, self.w_d1, xn, rxn, h, rh, subs)
            if kind == "own":
                S.dma("sp", h1v[:, :, 1024 * ti:1024 * ti + 1024], h[:, :, 0:1024], reads=[rh], writes=[R()])
            self.rmsnorm(h, rh, 1, xn, rxn, subs)
            blocks = [(1, kst)] + ([(0, qst)] if kind == "own" else [])
            for (bi, stg) in blocks:
                w, rw = self.load_w(self.w_in[bi], 8, 512)
                for (t0, n) in subs:
                    for cj in range(4):
                        ps, pr = S.bank()
                        for kc in range(8):
                            S.op("pe", lambda e, kc=kc, cj=cj, t0=t0, n=n, ps=ps, w=w: e.matmul(
                                ps[:, 0:n], w[:, kc, cj * 128:(cj + 1) * 128], xn[:, kc, t0:t0 + n],
                                start=(kc == 0), stop=(kc == 7)), reads=[rw, rxn], writes=[pr], track=(kc == 7))
                        S.op("act", lambda e, cj=cj, t0=t0, n=n, ps=ps, stg=stg: e.copy(out=stg[:, cj, t0:t0 + n], in_=ps[:, 0:n]),
                             reads=[pr], writes=[rst_])
                if bi == 1:
                    for (lt, off, n) in segs:
                        S.dma("sp", kv[:, :, lt:lt + n], stg[:, :, off:off + n], reads=[rst_], writes=[R()])
                else:
                    S.dma("sp", qv[:, :, 1024 * ti:1024 * ti + 1024], stg[:, :, 0:1024], reads=[rst_], writes=[R()])
            w, rw = self.load_w(self.w_in[2], 8, 512)
            ngrp = 8 if kind == "own" else 4
            for j in range(ngrp):
                ps, pr = S.bank()
                for kc in range(8):
                    S.op("pe", lambda e, kc=kc, j=j, ps=ps, w=w: e.matmul(
                        ps[:, 0:512], xn[:, kc, j * 128:(j + 1) * 128], w[:, kc, :],
                        start=(kc == 0), stop=(kc == 7)), reads=[rw, rxn], writes=[pr], track=(kc == 7))
                S.op("dve", lambda e, j=j, ps=ps: e.tensor_copy(out=vst[:, j, :], in_=ps[:, 0:512]),
                     reads=[pr], writes=[rst_])
            for (lt, off, n) in segs:
                g0 = off // 128
                S.dma("sp", self.v_s[lt:lt + n, :].rearrange("(j p) c -> p j c", p=128),
                      vst[:, g0:g0 + n // 128, :], reads=[rst_], writes=[R()])
            if kind == "own" and not STUB_SSM and not DBG.get("skip_sin"):
                w, rw = self.load_w(self.w_in[3], 8, 512)
                for ss in range(8):
                    ps, pr = S.bank()
                    for kc in range(8):
                        S.op("pe", lambda e, kc=kc, ss=ss, ps=ps, w=w: e.matmul(
                            ps[:, 0:512], xn[:, kc, ss:1024:8], w[:, kc, :],
                            start=(kc == 0), stop=(kc == 7)), reads=[rw, rxn], writes=[pr], track=(kc == 7))
                    S.op("act", lambda e, ss=ss, ps=ps: e.copy(
                        out=zt.rearrange("p a b -> p (a b)").rearrange("p (g s i) -> p g s i", s=8, i=16)[:, :, ss, :],
                        in_=ps[:, 0:512].rearrange("p (g i) -> p g i", i=16)), reads=[pr], writes=[rst_])
                for g4 in range(8):
                    pt, prt = self.psT[self.psT_n % 2]
                    self.psT_n += 1
                    for gg in range(4):
                        g = g4 * 4 + gg
                        S.op("pe", lambda e, g=g, gg=gg, pt=pt: e.transpose(
                            out=pt[:, gg * 128:(gg + 1) * 128], in_=zt.rearrange("p a b -> p (a b)")[:, g * 128:(g + 1) * 128],
                            identity=self.ident_bf), reads=[rst_, self.rC], writes=[prt], track=(gg == 3))
                    S.op("dve", lambda e, g4=g4, pt=pt, ti=ti: e.tensor_copy(
                        out=self.U_all[:, g4 * 4:g4 * 4 + 4, :, 16 * ti:16 * ti + 16],
                        in_=pt[:, 0:512].rearrange("p (g k s) -> p g s k", g=4, k=16, s=8)),
                         reads=[prt], writes=[self.rU])
        S.barrier(self.tok_s)


    def phase2(self):
        S = self.S
        A = self.A
        A.off = self.base_off
        allbanks = S.banks
        od_banks = allbanks[4:6]
        S.banks = allbanks[0:4]
        S.bnext = 0
        BM = A.take([8, 15 * 64], BF16)
        BM32 = A.take([8, 15 * 64], F32)
        onespad = A.take([2, 128], BF16)
        mcols = A.take([45], F32)
        rBM = R("BM")
        kb = [(A.take([4, 1024], BF16), R()) for _ in range(2)]
        vb = [(A.take([8, 512], BF16), R()) for _ in range(2)]
        Vp = [(A.take([8, 8, 128], BF16), R()) for _ in range(2)]
        qp = [(A.take([8, 512], BF16), R()) for _ in range(2)]
        stg = [(A.take([4, 512], BF16), R()) for _ in range(2)]
        PT = [(A.take([512], BF16), R()) for _ in range(3)]
        rec = [(A.take([256], F32), R()) for _ in range(2)]
        S.op("pool", lambda e: e.memset(BM32, MASKV / 8.0), writes=[rBM])
        S.op("pool", lambda e: e.memset(onespad, 0.0), writes=[rBM])
        S.op("pool", lambda e: e.memset(onespad[0:128, 0, 0:64], 1.0), writes=[rBM])
        S.op("pool", lambda e: e.memset(onespad[0:128, 1, 64:128], 1.0), writes=[rBM])
        S.dma("sp", mcols, self.maskcols[:, :], writes=[rBM])
        bm4 = BM32.rearrange("p h (s c) -> p (h s) c", c=64)
        rpbv = self.rpb.rearrange("h r c -> (h r) c")
        for cq in range(64):
            cs = min(max(cq - 8, 0), 48)
            o = cs - cq + 15
            S.dma("sp", bm4[cq:cq + 1, :, cs:cs + 16], rpbv[:, o:o + 16].unsqueeze(0), reads=[rBM], writes=[rBM])
        S.op("dve", lambda e: e.memset(BM[64:128, :, :], 0.0), writes=[rBM])
        S.op("dve", lambda e: e.tensor_scalar(out=BM[0:64, :, :], in0=BM32[0:64, :, :], scalar1=8.0, scalar2=None,
                                              op0=ALU.mult), reads=[rBM], writes=[rBM])
        for i in range(2):
            S.op("pool", lambda e, i=i: e.memset(Vp[i][0], 0.0), writes=[Vp[i][1]])
            S.op("pool", lambda e, i=i: e.memset(qp[i][0], 0.0), writes=[qp[i][1]])
        kv = self.kT_s.rearrange("(c p) t -> p c t", p=128)
        qv = self.qT_s.rearrange("(h2 par d) t -> par d h2 t", par=2, d=64)
        av = self.attnT_s.rearrange("(c p) t -> p c t", p=128)
        specials = [4, 5, 6, 7, 65, 66, 67]
        ptn = 0
        for b in range(8):
            bi = b % 2
            kt, rk = kb[bi]
            vt, rv = vb[bi]
            vp, rvp = Vp[bi]
            qt, rq = qp[bi]
            sg, rsg = stg[bi]
            S.dma("sp", kt, kv[:, :, 512 * b:512 * b + 1024], writes=[rk])
            S.dma("sp", vt, self.v_s[512 * b:512 * b + 1024, :].rearrange("(j p) c -> p j c", p=128), writes=[rv])
            for par in range(2):
                S.dma("sp", qt[par * 64:(par + 1) * 64, par::2, :], qv[par, :, :, 512 * b:512 * b + 512], writes=[rq])
                S.op("pool", lambda e, par=par, vp=vp, vt=vt: e.tensor_copy(
                    out=vp[:, :, par::2, par * 64:(par + 1) * 64],
                    in_=vt.rearrange("p j (h2 q d) -> p j h2 q d", q=2, d=64)[:, :, :, par, :]),
                     reads=[rv], writes=[rvp])
            items = []
            for lr in range(8):
                l = 4 + 8 * b + lr
                if l in specials:
                    si = specials.index(l)
                    pi0 = 0 if l < 8 else 30
                    prs = [(pi0 + j, 3 + si * 6 + j) for j in range(6)]
                else:
                    p_lo, p_hi = (l - 4) // 2, (l + 3) // 2
                    prs = []
                    for pi in range(p_lo, p_hi + 1):
                        v = 0
                        if l % 2 == 1 and pi == p_lo:
                            v = 1
                        if l % 2 == 1 and pi == p_hi:
                            v = 2
                        prs.append((pi, v))
                for j, (pi, v) in enumerate(prs):
                    items.append((lr, l, pi, v, j == 0, j == len(prs) - 1))

            def emit_qk(it):
                lr, l, pi, v, first, last = it
                st, rst = S.bank()
                ko = (pi - 4 * b) * 128
                slot = 2 * pi - l + 7
                for hh in range(8):
                    S.op("pe", lambda e, hh=hh, st=st, ko=ko, lr=lr, kt=kt, qt=qt: e.matmul(
                        st[:, hh * 64:(hh + 1) * 64], kt[:, hh // 2, ko:ko + 128], qt[:, hh, 64 * lr:64 * lr + 64],
                        start=(hh == 0), stop=False), reads=[rk, rq], writes=[rst], track=False)
                    S.op("pe", lambda e, hh=hh, st=st, slot=slot: e.matmul(
                        st[:, hh * 64:(hh + 1) * 64], BM[:, hh, slot * 64:(slot + 2) * 64], self.ident_bf[:, 0:64],
                        start=False, stop=True), reads=[rBM, self.rC], writes=[rst], track=(hh == 7))
                return st, rst

            cur = emit_qk(items[0])
            od = None
            for ii, it in enumerate(items):
                lr, l, pi, v, first, last = it
                nxt = emit_qk(items[ii + 1]) if ii + 1 < len(items) else None
                st, rst = cur
                pt, rpt = PT[ptn % 3]
                ptn += 1
                S.op("act", lambda e, st=st, pt=pt, v=v: e.activation(out=pt, in_=st, func=AF.Exp,
                                                                     bias=mcols[:, v:v + 1], scale=0.125),
                     reads=[rst, rBM], writes=[rpt])
                if first:
                    od = od_banks[(8 * b + lr) % 2]
                odt, rod = od
                pl = pi - 4 * b
                for hh in range(8):
                    h2 = hh // 2
                    S.op("pe", lambda e, hh=hh, h2=h2, odt=odt, pl=pl, pt=pt, first=first, last=last, vp=vp: e.matmul(
                        odt[:, h2 * 64:(h2 + 1) * 64], vp[:, pl, hh, :], pt[:, hh * 64:(hh + 1) * 64],
                        start=(first and hh == 0), stop=last), reads=[rvp, rpt], writes=[rod], track=False)
                    S.op("pe", lambda e, hh=hh, h2=h2, odt=odt, pt=pt, last=last: e.matmul(
                        odt[:, 256 + h2 * 64:256 + (h2 + 1) * 64], onespad[:, hh % 2, :], pt[:, hh * 64:(hh + 1) * 64],
                        start=False, stop=last), reads=[rBM, rpt], writes=[rod], track=(hh == 7))
                if last:
                    rc, rrc = rec[(8 * b + lr) % 2]
                    S.op("dve", lambda e, odt=odt, rc=rc: e.reciprocal(out=rc, in_=odt[:, 256:512]),
                         reads=[rod], writes=[rrc])
                    S.op("dve", lambda e, odt=odt, rc=rc, lr=lr, sg=sg: e.tensor_tensor(
                        out=sg[:, :, 64 * lr:64 * lr + 64], in0=odt[:, 0:256].rearrange("p (h q) -> p h q", q=64),
                        in1=rc.rearrange("p (h q) -> p h q", q=64), op=ALU.mult), reads=[rod, rrc], writes=[rsg])
                cur = nxt
            S.dma("sp", av[:, :, 512 * b:512 * b + 512], sg, reads=[rsg], writes=[R()])
        S.banks = allbanks
        S.bnext = 0
        S.barrier(self.tok_s)


    def tt(self, eng, out, a, b, op, reads, writes):
        self.S.op(eng, lambda e: e.tensor_tensor(out=out, in0=a, in1=b, op=op), reads=reads, writes=writes)

    def phase3a(self):
        S = self.S
        A = self.A
        A.off = self.base_off
        PI = float(np.pi)
        NQ = 64
        prm = A.take([3, 64], F32, parts=64)
        cc = A.take([2, 64, 16], F32, parts=64)
        E = A.take([2, 64, 65], F32, parts=64)
        a64 = A.take([2, 64], F32, parts=64)
        self.gall_off = A.off
        self.Gall = A.take([2, 64, 64], F32, parts=64)
        self.xp_off = A.off
        self.XP = A.take([2, 64, 64], F32, parts=64)
        self.Xin = A.take([2, 64], F32, parts=64)
        self.rt1 = A.take([64], F32, parts=64)
        self.rt2 = A.take([64], F32, parts=64)
        self.rrt, self.rrt2 = R(), R()
        sel = A.take([16], F32, parts=64)
        self.ssm_keep = A.off
        bb = A.take([2, 64, 16], F32, parts=64)
        Bb = A.take([2, 64, 16], F32, parts=64)
        self.E, self.cc_, self.a64 = E, cc, a64
        rP, rE, rB, rG, rX = R("prm"), R("E"), R("Bb"), R("G"), R("XP")
        self.rE, self.rP, self.rG, self.rX = rE, rP, rG, rX
        expo = A.take([65], F32, parts=64)
        dd = A.take([512], F32, parts=1)
        eye16 = A.take([256], F32, parts=1)
        ind0 = A.take([64], F32, parts=64)
        self.sel_sb = sel
        mid = A.off
        sc3 = A.take([64, 65], F32, parts=64)
        A.off = self.gall_off
        sc = [A.take([64, 65], F32, parts=64) for _ in range(3)] + [sc3]
        A.off = mid
        rS = [R() for _ in range(4)]
        S.dma("sp", prm, self.ssm_p[:, :, :], writes=[rP])
        S.dma("sp", bb, self.ssm_b[:, :, :, :], writes=[rP])
        S.dma("sp", cc, self.ssm_c[:, :, :, :], writes=[rP])
        S.dma("sp", expo, self.cst_expo[:, :], writes=[rP])
        S.dma("sp", dd, self.ssm_dd[:, :], writes=[rP])
        S.dma("sp", eye16, self.cst_eye16[:, :], writes=[rP])
        S.dma("sp", sel, self.sel[:, :], writes=[rP])
        S.op("dve", lambda e: e.memset(ind0, 0.0), writes=[rB])
        S.op("dve", lambda e: e.memset(ind0[:, 0:1], 1.0), writes=[rB])
        lre, lim, ldt = prm[:, 0, :], prm[:, 1, :], prm[:, 2, :]
        S.op("act", lambda e: e.activation(out=ldt, in_=ldt, func=AF.Exp), reads=[rP], writes=[rP])
        S.op("dve", lambda e: e.tensor_scalar(out=lre, in0=lre, scalar1=-1e-4, scalar2=None, op0=ALU.min), reads=[rP], writes=[rP])
        al, th = a64[:, 0, :], a64[:, 1, :]
        self.tt("dve", al, lre, ldt, ALU.mult, [rP], [rE])
        self.tt("dve", th, lim, ldt, ALU.mult, [rP], [rE])
        al3 = al.rearrange("p (q o) -> p q o", o=1).to_broadcast([64, 64, 65])
        th3 = th.rearrange("p (q o) -> p q o", o=1).to_broadcast([64, 64, 65])
        ex3 = expo.rearrange("p (o n) -> p o n", o=1).to_broadcast([64, 64, 65])
        mag, ang, t0_, t1_ = sc
        self.tt("dve", mag, al3, ex3, ALU.mult, [rE, rP], [rS[0]])
        S.op("act", lambda e: e.activation(out=mag, in_=mag, func=AF.Exp), reads=[rS[0]], writes=[rS[0]])
        self.tt("dve", ang, th3, ex3, ALU.mult, [rE, rP], [rS[1]])
        angi = t1_.bitcast(mybir.dt.int32)
        S.op("dve", lambda e: e.tensor_scalar(out=t0_, in0=ang, scalar1=1.0 / (2 * PI), scalar2=None, op0=ALU.mult), reads=[rS[1]], writes=[rS[2]])
        S.op("dve", lambda e: e.tensor_copy(out=angi, in_=t0_), reads=[rS[2]], writes=[rS[3]])
        S.op("dve", lambda e: e.tensor_copy(out=ang, in_=angi), reads=[rS[3]], writes=[rS[1]])
        self.tt("dve", t0_, t0_, ang, ALU.subtract, [rS[1], rS[2]], [rS[2]])
        S.op("dve", lambda e: e.tensor_scalar(out=ang, in0=t0_, scalar1=2 * PI, scalar2=None, op0=ALU.mult), reads=[rS[2]], writes=[rS[1]])

        def wrap(x, rx):
            S.op("dve", lambda e: e.tensor_scalar(out=t0_, in0=x, scalar1=PI, scalar2=-2 * PI, op0=ALU.is_gt, op1=ALU.mult), reads=[rx], writes=[rS[2]])
            self.tt("dve", x, x, t0_, ALU.add, [rS[2], rx], [rx])
            S.op("dve", lambda e: e.tensor_scalar(out=t0_, in0=x, scalar1=-PI, scalar2=2 * PI, op0=ALU.is_lt, op1=ALU.mult), reads=[rx], writes=[rS[2]])
            self.tt("dve", x, x, t0_, ALU.add, [rS[2], rx], [rx])
        wrap(ang, rS[1])
        S.op("dve", lambda e: e.tensor_scalar(out=t1_, in0=ang, scalar1=PI / 2, scalar2=None, op0=ALU.add), reads=[rS[1]], writes=[rS[3]])
        wrap(t1_, rS[3])
        S.op("act", lambda e: e.activation(out=ang, in_=ang, func=AF.Sin), reads=[rS[1]], writes=[rS[1]])
        S.op("act", lambda e: e.activation(out=t1_, in_=t1_, func=AF.Sin), reads=[rS[3]], writes=[rS[3]])
        self.tt("dve", E[:, 0, :, :], mag, t1_, ALU.mult, [rS[0], rS[3]], [rE])
        self.tt("dve", E[:, 1, :, :], mag, ang, ALU.mult, [rS[0], rS[1]], [rE])
        ar1, ai, den, cr, ci, u1 = (mag[:, 0, 0:64], mag[:, 1, 0:64], mag[:, 2, 0:64], mag[:, 3, 0:64], mag[:, 4, 0:64], mag[:, 5, 0:64])
        rT = rS[0]
        S.op("dve", lambda e: e.tensor_scalar(out=ar1, in0=E[:, 0, :, 1], scalar1=-1.0, scalar2=None, op0=ALU.add), reads=[rE], writes=[rT])
        S.op("dve", lambda e: e.tensor_copy(out=ai, in_=E[:, 1, :, 1]), reads=[rE], writes=[rT])
        self.tt("dve", den, lre, lre, ALU.mult, [rP, rT], [rT])
        self.tt("dve", u1, lim, lim, ALU.mult, [rP, rT], [rT])
        self.tt("dve", den, den, u1, ALU.add, [rT], [rT])
        S.op("dve", lambda e: e.reciprocal(out=den, in_=den), reads=[rT], writes=[rT])
        self.tt("dve", cr, ar1, lre, ALU.mult, [rT, rP], [rT])
        self.tt("dve", u1, ai, lim, ALU.mult, [rT, rP], [rT])
        self.tt("dve", cr, cr, u1, ALU.add, [rT], [rT])
        self.tt("dve", cr, cr, den, ALU.mult, [rT], [rT])
        self.tt("dve", ci, ai, lre, ALU.mult, [rT, rP], [rT])
        self.tt("dve", u1, ar1, lim, ALU.mult, [rT, rP], [rT])
        self.tt("dve", ci, ci, u1, ALU.subtract, [rT], [rT])
        self.tt("dve", ci, ci, den, ALU.mult, [rT], [rT])
        S.op("dve", lambda e: e.tensor_copy(out=a64, in_=E[:, :, :, 64]), reads=[rE], writes=[rE])
        cr3 = cr.rearrange("p (q o) -> p q o", o=1).to_broadcast([64, 64, 16])
        ci3 = ci.rearrange("p (q o) -> p q o", o=1).to_broadcast([64, 64, 16])
        w1 = ang[:, :, 0:16]
        self.tt("dve", Bb[:, 0], cr3, bb[:, 0], ALU.mult, [rT, rP], [rB])
        self.tt("dve", w1, ci3, bb[:, 1], ALU.mult, [rT, rP], [rS[1]])
        self.tt("dve", Bb[:, 0], Bb[:, 0], w1, ALU.subtract, [rS[1], rB], [rB])
        self.tt("dve", Bb[:, 1], cr3, bb[:, 1], ALU.mult, [rT, rP], [rB])
        self.tt("dve", w1, ci3, bb[:, 0], ALU.mult, [rT, rP], [rS[1]])
        self.tt("dve", Bb[:, 1], Bb[:, 1], w1, ALU.add, [rS[1], rB], [rB])
        if DBG.get("dump"):
            S.dma("sp", self.dbg_E[:, :], E.rearrange("p c q n -> p (c q n)"), reads=[rE], writes=[R()])
            S.dma("sp", self.dbg_Bb[:, :], Bb.rearrange("p c q i -> p (c q i)"), reads=[rB], writes=[R()])
            S.dma("sp", self.dbg_U[:, :], self.U_all[:, 0, :, :].rearrange("p s k -> p (s k)"), reads=[self.rU], writes=[R()])
        S.barrier(self.tok_s)
        if DBG.get("p3_stop", 99) <= 1:
            return
        A.off = mid
        Wt1 = (A.take([2, 2, 2, 256], F32, parts=64), R())
        kst = [(A.take([2, 256], F32, parts=64), R()) for _ in range(2)]
        Erev = A.take([2, 32, 64], F32, parts=64)
        Dfull = A.take([2, 256], F32, parts=64)
        tmpw_all = A.take([512], F32, parts=64)
        rDf = R()
        S.op("dve", lambda e: e.memset(Dfull, 0.0), writes=[rDf])
        S.op("dve", lambda e: e.tensor_copy(out=Erev, in_=E[:, :, 32:64, 63::-1]), reads=[rE], writes=[rB])
        ktv = self.ktab_s.rearrange("g i n o -> g n i o")
        NG = 2
        for g8 in range(16):
            wt, rw = Wt1
            for d in range(2):
                q0 = 32 * d + NG * g8
                Br = Bb[:, 0, q0:q0 + NG, :].rearrange("p g (i o) -> p g i o", o=1).to_broadcast([64, NG, 16, 16])
                Bi = Bb[:, 1, q0:q0 + NG, :].rearrange("p g (i o) -> p g i o", o=1).to_broadcast([64, NG, 16, 16])
                Cr = cc[:, 0, q0:q0 + NG, :].rearrange("p g (i o) -> p g i o", i=1).to_broadcast([64, NG, 16, 16])
                Ci = cc[:, 1, q0:q0 + NG, :].rearrange("p g (i o) -> p g i o", i=1).to_broadcast([64, NG, 16, 16])
                wr = wt[:, d, 0].rearrange("p g (i o) -> p g i o", o=16)
                wi = wt[:, d, 1].rearrange("p g (i o) -> p g i o", o=16)
                tmpw = tmpw_all.rearrange("p (g i o) -> p g i o", g=NG, i=16)
                self.tt("dve", wr, Br, Cr, ALU.mult, [rB, rP], [rw])
                self.tt("dve", tmpw, Bi, Ci, ALU.mult, [rB, rP], [rS[2]])
                self.tt("dve", wr, wr, tmpw, ALU.subtract, [rS[2], rw], [rw])
                self.tt("dve", wi, Br, Ci, ALU.mult, [rB, rP], [rw])
                self.tt("dve", tmpw, Bi, Cr, ALU.mult, [rB, rP], [rS[2]])
                self.tt("dve", wi, wi, tmpw, ALU.add, [rS[2], rw], [rw])
                S.op("dve", lambda e, wi=wi: e.tensor_scalar(out=wi, in0=wi, scalar1=-1.0, scalar2=None, op0=ALU.mult), reads=[rw], writes=[rw])
            gsl = slice(NG * g8 * 16, NG * (g8 + 1) * 16)
            self.tt("dve", Dfull[0:1].rearrange("p g (i o) -> p g i o", o=16),
                    eye16.rearrange("p (g i o) -> p g i o", g=1, o=16).to_broadcast([1, NG, 16, 16]),
                    dd[:, gsl].rearrange("p (g i o) -> p g i o", i=16, o=1).to_broadcast([1, NG, 16, 16]), ALU.mult, [rP], [rDf])
            for gl in range(NG):
                g = g8 * NG + gl
                ks, rks = kst[g % 2]
                psf, rpf = S.bank()
                S.op("pe", lambda e, g=g, gl=gl, psf=psf, wt=wt: e.matmul(psf[0:64, 0:256], E[:, 0, g, 0:64], wt[:, 0, 0, gl, :], start=True, stop=False), reads=[rE, rw], writes=[rpf], track=False)
                S.op("pe", lambda e, g=g, gl=gl, psf=psf, wt=wt: e.matmul(psf[0:64, 0:256], E[:, 1, g, 0:64], wt[:, 0, 1, gl, :], start=False, stop=False), reads=[rE, rw], writes=[rpf], track=False)
                S.op("pe", lambda e, gl=gl, psf=psf, wt=wt: e.matmul(psf[0:64, 0:256], ind0, wt[:, 1, 0, gl, :], start=False, stop=False), reads=[rB, rw], writes=[rpf], track=False)
                S.op("pe", lambda e, gl=gl, psf=psf: e.matmul(psf[0:64, 0:256], ind0, Dfull[:, gl, :], start=False, stop=True), reads=[rB, rDf], writes=[rpf])
                psb, rpb = S.bank()
                S.op("pe", lambda e, g=g, gl=gl, psb=psb, wt=wt: e.matmul(psb[0:64, 0:256], Erev[:, 0, g, :], wt[:, 1, 0, gl, :], start=True, stop=False), reads=[rB, rw], writes=[rpb], track=False)
                S.op("pe", lambda e, g=g, gl=gl, psb=psb, wt=wt: e.matmul(psb[0:64, 0:256], Erev[:, 1, g, :], wt[:, 1, 1, gl, :], start=False, stop=True), reads=[rB, rw], writes=[rpb])
                S.op("act", lambda e, psf=psf, ks=ks: e.copy(out=ks[:, 0, :], in_=psf[0:64, 0:256]), reads=[rpf], writes=[rks])
                S.op("act", lambda e, psb=psb, ks=ks: e.copy(out=ks[:, 1, :], in_=psb[0:64, 0:256]), reads=[rpb], writes=[rks])
                S.dma("sp", ktv[g, 63:127, :, :], ks[:, 0, :].rearrange("p (i o) -> p i o", o=16), reads=[rks], writes=[R()])
                S.dma("sp", ktv[g, 0:63, :, :], ks[0:63, 1, :].rearrange("p (i o) -> p i o", o=16), reads=[rks], writes=[R()])
        S.barrier(self.tok_s)
        if DBG.get("p3_stop", 99) <= 2:
            return
        A.off = mid
        PGw = [(A.take([2, 64, 16], F32, parts=64), R()) for _ in range(2)]
        tw = [(A.take([64, 16], F32, parts=64), R()) for _ in range(2)]
        PGT = [(A.take([8, 2, 64], BF16), R()) for _ in range(2)]
        for q in range(NQ):
            d, g = q // 32, q % 32
            pg, rpg = PGw[q % 2]
            pgt, rpgt = PGT[q % 2]
            if d == 0:
                Esr, Esi = E[:, 0, q, 63::-1], E[:, 1, q, 63::-1]
            else:
                Esr, Esi = E[:, 0, q, 0:64], E[:, 1, q, 0:64]
            Esr = Esr.rearrange("p (s o) -> p s o", o=1).to_broadcast([64, 64, 16])
            Esi = Esi.rearrange("p (s o) -> p s o", o=1).to_broadcast([64, 64, 16])
            Bqr = Bb[:, 0, q, :].rearrange("p (o i) -> p o i", o=1).to_broadcast([64, 64, 16])
            Bqi = Bb[:, 1, q, :].rearrange("p (o i) -> p o i", o=1).to_broadcast([64, 64, 16])
            self.tt("dve", pg[:, 0], Esr, Bqr, ALU.mult, [rE, rB], [rpg])
            self.tt("dve", tw[0][0], Esi, Bqi, ALU.mult, [rE, rB], [tw[0][1]])
            self.tt("dve", pg[:, 0], pg[:, 0], tw[0][0], ALU.subtract, [tw[0][1], rpg], [rpg])
            self.tt("pool", pg[:, 1], Esr, Bqi, ALU.mult, [rE, rB], [rpg])
            self.tt("pool", tw[1][0], Esi, Bqr, ALU.mult, [rE, rB], [tw[1][1]])
            self.tt("pool", pg[:, 1], pg[:, 1], tw[1][0], ALU.add, [tw[1][1], rpg], [rpg])
            for hb in range(2):
                ps, pr = S.bank()
                for s4 in range(4):
                    ss = hb * 4 + s4
                    for c in range(2):
                        S.op("pe", lambda e, ss=ss, s4=s4, c=c, ps=ps, pg=pg: e.transpose(
                            out=ps[:, (s4 * 2 + c) * 64:(s4 * 2 + c + 1) * 64], in_=pg[:, c].rearrange("p s i -> p (s i)")[:, 128 * ss:128 * ss + 128],
                            identity=self.ident_f[0:64, 0:64]), reads=[rpg, self.rC], writes=[pr], track=(s4 == 3 and c == 1))
                S.op("act", lambda e, hb=hb, ps=ps, pgt=pgt: e.copy(
                    out=pgt[:, hb * 4:hb * 4 + 4, :, :], in_=ps[:, 0:512].rearrange("p (s c x) -> p s c x", c=2, x=64)),
                     reads=[pr], writes=[rpgt])
            ps, pr = S.bank()
            for c in range(2):
                for ss in range(8):
                    S.op("pe", lambda e, ss=ss, c=c, ps=ps, pgt=pgt, g=g: e.matmul(
                        ps[0:64, c * 64:(c + 1) * 64], pgt[:, ss, c, :], self.U_all[:, g, ss, :],
                        start=(c == 0 and ss == 0), stop=(ss == 7)), reads=[rpgt, self.rU], writes=[pr], track=(c == 1 and ss == 7))
            if d == 0:
                S.op("act", lambda e, ps=ps, q=q: e.copy(out=self.Gall[:, :, q, :], in_=ps[0:64, 0:128].rearrange("p (c k) -> p c k", c=2)),
                     reads=[pr], writes=[rG])
            else:
                S.op("act", lambda e, ps=ps, q=q: e.copy(out=self.Gall[:, :, q, 63::-1], in_=ps[0:64, 0:128].rearrange("p (c k) -> p c k", c=2)),
                     reads=[pr], writes=[rG])
        if DBG.get("dump"):
            S.dma("sp", self.dbg_G[:, :], self.Gall.rearrange("p c q k -> p (c q k)"), reads=[rG], writes=[R()])
            S.dma("sp", self.dbg_kt[:, :], self.ktab_s[0].rearrange("i n o -> i (n o)"), writes=[R()])
        if DBG.get("p3_stop", 99) <= 3:
            S.barrier(self.tok_s)
            return
        S.op("dve", lambda e: e.memset(self.Xin, 0.0), writes=[rX])
        self.recur()
        xend = A.take([2, 64], F32, parts=64)
        self.cstep(xend[:, 0, :], xend[:, 1, :], self.XP[:, 0, :, 63], self.XP[:, 1, :, 63], self.Gall[:, 0, :, 63], self.Gall[:, 1, :, 63])
        rxe = R()
        S.dma("sp", self.cc_src.ap()[:, :], xend.rearrange("p c q -> p (c q)"), reads=[rX], writes=[rxe])
        S._deps("pool", [rxe], [])
        src, dst, ccsem = self.cc_src, self.cc_dst, S.ccsem
        S.prog["pool"].append(lambda E_: E_.collective_compute(
            "AllGather", ALU.bypass, replica_groups=[list(range(NCORES))], ins=[src.ap()[:, :]], outs=[dst.ap()[:, :]]).then_inc(ccsem))
        self.cc_ev = ("CC", ccsem, 1)
        S.barrier(self.tok_s)


    def phase3b(self):
        S = self.S
        A = self.A
        A.off = self.ssm_keep
        E, cc = self.E, self.cc_
        rE, rP, rG, rX = self.rE, self.rP, self.rG, self.rX
        gath = A.take([NCORES, 128], F32, parts=64)
        rg = R()
        S._wait("sp", self.cc_ev)
        S.dma("sp", gath, self.cc_dst.ap().rearrange("(r p) x -> p r x", p=64), writes=[rg])
        for dirn in range(2):
            xo = self.Xin[:, :, 32 * dirn:32 * dirn + 32]
            for r in range(NCORES):
                gi = gath[:, r, :].rearrange("p (c q) -> p c q", c=2)[:, :, 32 * dirn:32 * dirn + 32]
                sc_ = self.sel_sb[:, dirn * 8 + r:dirn * 8 + r + 1]
                if r == 0:
                    S.op("dve", lambda e, xo=xo, gi=gi, sc_=sc_: e.tensor_scalar(out=xo, in0=gi, scalar1=sc_, scalar2=None, op0=ALU.mult),
                         reads=[rg, rP], writes=[rX])
                else:
                    S.op("dve", lambda e, xo=xo, gi=gi, sc_=sc_: e.scalar_tensor_tensor(out=xo, in0=gi, scalar=sc_, in1=xo, op0=ALU.mult, op1=ALU.add),
                         reads=[rg, rP, rX], writes=[rX])
        self.recur()
        XPb = A.take([2, 64, 64], BF16, parts=64)
        rXb = R()
        S.op("dve", lambda e: e.tensor_copy(out=XPb[:, :, 0:32, :], in_=self.XP[:, :, 0:32, :]), reads=[rX], writes=[rXb])
        S.op("dve", lambda e: e.tensor_copy(out=XPb[:, :, 32:64, :], in_=self.XP[:, :, 32:64, 63::-1]), reads=[rX], writes=[rXb])
        S.op("dve", lambda e: e.tensor_scalar(out=cc[:, 1], in0=cc[:, 1], scalar1=-1.0, scalar2=None, op0=ALU.mult), reads=[rP], writes=[rP])
        if DBG.get("dump"):
            S.dma("sp", self.dbg_XP[:, :], self.XP.rearrange("p c q k -> p (c q k)"), reads=[rX], writes=[R()])
        S.barrier(self.tok_s)
        if DBG.get("p3_stop", 99) <= 5:
            return
        save = A.off
        A.off = self.gall_off
        ysT = A.take([4, 4096], BF16)
        A.off = self.xp_off
        Ysb = A.take([64, 128], BF16, parts=64)
        A.off = save
        rys, rYsb = R(), R()
        Mt = [(A.take([1920], BF16), R()) for _ in range(2)]
        cxs = [[(A.take([2, 64, 16], BF16, parts=64), R())] for _ in range(2)]
        tws = [[(A.take([64, 16], F32, parts=64), R()) for _ in range(2)] for _ in range(2)]
        wglu = A.take([4, 512], BF16)
        rwg = R()
        S.dma("pool", wglu, self.w_glu.rearrange("p (k m) -> p k m", m=512), writes=[rwg])
        stg = [(A.take([512], BF16), R()) for _ in range(2)]
        tmpf = [(A.take([512], F32), R()) for _ in range(2)]
        for g8 in range(4):
            for gl in range(8):
                g = g8 * 8 + gl
                mt, rmt = Mt[g % 2]
                for ss in range(8):
                    S.dma("pool", mt[ss * 16:(ss + 1) * 16, :],
                          self.ktab_s[g, :, 7 - ss:127 - ss, :].rearrange("i n o -> i (n o)"), writes=[rmt])
                for dirn in range(2):
                    q = 32 * dirn + g
                    cx, rcx = cxs[dirn][0]
                    eng = "dve" if dirn == 0 else "pool"
                    (ta, rta), (tb, rtb) = tws[dirn]
                    if dirn == 0:
                        Etr, Eti = E[:, 0, q, 1:65], E[:, 1, q, 1:65]
                    else:
                        Etr, Eti = E[:, 0, q, 64:0:-1], E[:, 1, q, 64:0:-1]
                    Etr = Etr.rearrange("p (t o) -> p t o", o=1).to_broadcast([64, 64, 16])
                    Eti = Eti.rearrange("p (t o) -> p t o", o=1).to_broadcast([64, 64, 16])
                    Cr = cc[:, 0, q, :].rearrange("p (t o) -> p t o", t=1).to_broadcast([64, 64, 16])
                    nCi = cc[:, 1, q, :].rearrange("p (t o) -> p t o", t=1).to_broadcast([64, 64, 16])
                    self.tt(eng, ta, Etr, Cr, ALU.mult, [rE, rP], [rta])
                    self.tt(eng, tb, Eti, nCi, ALU.mult, [rE, rP], [rtb])
                    self.tt(eng, cx[:, 0], ta, tb, ALU.add, [rta, rtb], [rcx])
                    self.tt(eng, ta, Etr, nCi, ALU.mult, [rE, rP], [rta])
                    self.tt(eng, tb, Eti, Cr, ALU.mult, [rE, rP], [rtb])
                    self.tt(eng, cx[:, 1], ta, tb, ALU.subtract, [rta, rtb], [rcx])
                (pa, rpa) = S.bank()
                (pb, rpb) = S.bank()
                pbk = [(pa, rpa), (pb, rpb)]
                firsts = [True, True]
                for dirn in range(2):
                    q = 32 * dirn + g
                    cx, rcx = cxs[dirn][0]
                    for c in range(2):
                        for hb in range(2):
                            pk, rpk = pbk[hb]
                            S.op("pe", lambda e, pk=pk, c=c, q=q, cx=cx, hb=hb, st=firsts[hb]: e.matmul(
                                pk[0:64, 0:512], XPb[:, c, q, :], cx[:, c].rearrange("p t o -> p (t o)")[:, hb * 512:(hb + 1) * 512],
                                start=st, stop=False), reads=[rXb, rcx], writes=[rpk], track=False)
                            firsts[hb] = False
                for T in range(8):
                    pk, rpk = pbk[T // 4]
                    for ss in range(8):
                        last = (T % 4 == 3 and ss == 7)
                        S.op("pe", lambda e, pk=pk, T=T, ss=ss, g=g, mt=mt, last=last: e.matmul(
                            pk[0:64, (T % 4) * 128:(T % 4 + 1) * 128], self.U_all[:, g, ss, :],
                            mt[:, (T - ss + 7) * 128:(T - ss + 8) * 128], start=False, stop=last),
                             reads=[self.rU, rmt], writes=[rpk], track=last)
                for hb in range(2):
                    pk, rpk = pbk[hb]
                    S.op("act", lambda e, pk=pk, hb=hb, gl=gl: e.copy(
                        out=Ysb[:, hb * 32:(hb + 1) * 32, gl * 16:(gl + 1) * 16],
                        in_=pk[0:64, 0:512].rearrange("p (t o) -> p t o", o=16)), reads=[rpk], writes=[rYsb])
            if DBG.get("dump") and g8 == 0:
                S.dma("sp", self.dbg_Y[:, :], Ysb.rearrange("p t c -> p (t c)"), reads=[rYsb], writes=[R()])
            if DBG.get("p3_stop", 99) <= 6:
                continue
            for t8 in range(8):
                pt, prt = self.psT[self.psT_n % 2]
                self.psT_n += 1
                for t_ in range(8):
                    t = t8 * 8 + t_
                    S.op("pe", lambda e, t=t, t_=t_, pt=pt: e.transpose(
                        out=pt[:, t_ * 64:(t_ + 1) * 64], in_=Ysb[:, t, :], identity=self.ident_bf[0:64, 0:64]),
                         reads=[rYsb, self.rC], writes=[prt], track=(t_ == 7))
                xg, rxg = tmpf[0]
                x2, rx2 = tmpf[1]
                S.op("act", lambda e, pt=pt, xg=xg: e.copy(out=xg, in_=pt[:, 0:512]), reads=[prt], writes=[rxg])
                S.op("act", lambda e, pt=pt, x2=x2: e.activation(out=x2, in_=pt[:, 0:512], func=AF.Square), reads=[prt], writes=[rx2])
                S.op("dve", lambda e, x2=x2: e.tensor_scalar(out=x2, in0=x2, scalar1=0.044715, scalar2=1.0, op0=ALU.mult, op1=ALU.add),
                     reads=[rx2], writes=[rx2])
                self.tt("dve", x2, x2, xg, ALU.mult, [rx2, rxg], [rx2])
                S.op("act", lambda e, x2=x2: e.activation(out=x2, in_=x2, func=AF.Sigmoid, scale=1.5957691216), reads=[rx2], writes=[rx2])
                self.tt("dve", ysT[:, g8, :].rearrange("p (k t) -> p t k", t=64)[:, t8 * 8:(t8 + 1) * 8, :],
                        xg.rearrange("p (t k) -> p t k", k=64), x2.rearrange("p (t k) -> p t k", k=64), ALU.mult,
                        [rxg, rx2], [rys])
        if DBG.get("p3_stop", 99) <= 7:
            S.barrier(self.tok_s)
            return
        yv = self.yssmT_s.rearrange("(c p) t -> p c t", p=128)
        n = 0
        for oc in range(4):
            for tt_ in range(8):
                ps, pr = S.bank()
                for kc in range(4):
                    S.op("pe", lambda e, kc=kc, oc=oc, tt_=tt_, ps=ps: e.matmul(
                        ps[:, 0:512], wglu[:, kc, oc * 128:(oc + 1) * 128], ysT[:, kc, tt_ * 512:(tt_ + 1) * 512],
                        start=(kc == 0), stop=(kc == 3)), reads=[rwg, rys], writes=[pr], track=(kc == 3))
                tm, rtm = tmpf[n % 2]
                sg, rsg = stg[n % 2]
                n += 1
                S.op("act", lambda e, ps=ps, tm=tm, oc=oc: e.activation(out=tm, in_=ps[:, 0:512], func=AF.Sigmoid,
                                                                       bias=self.glub_sb[:, oc:oc + 1], scale=1.0),
                     reads=[pr, self.rC], writes=[rtm])
                S.op("dve", lambda e, tm=tm, sg=sg, oc=oc, tt_=tt_: e.tensor_tensor(
                    out=sg, in0=tm, in1=ysT[:, oc, tt_ * 512:(tt_ + 1) * 512], op=ALU.mult), reads=[rtm, rys], writes=[rsg])
                S.dma("sp", yv[:, oc, tt_ * 512:(tt_ + 1) * 512], sg, reads=[rsg], writes=[R()])
        S.barrier(self.tok_s)

    def cstep(self, outr, outi, xr, xi, gr, gi):
        ar, ai = self.a64[:, 0, :], self.a64[:, 1, :]
        u1, u2 = self.rt1, self.rt2
        rX, rG, rE = self.rX, self.rG, self.rE
        self.tt("dve", u1, ar, xr, ALU.mult, [rX, rE], [self.rrt])
        self.tt("dve", u2, ai, xi, ALU.mult, [rX, rE], [self.rrt])
        self.tt("dve", u1, u1, u2, ALU.subtract, [self.rrt], [self.rrt])
        self.tt("dve", u2, ar, xi, ALU.mult, [rX, rE], [self.rrt2])
        self.tt("dve", outr, u1, gr, ALU.add, [self.rrt, rG], [rX])
        self.tt("dve", u1, ai, xr, ALU.mult, [rX, rE], [self.rrt])
        self.tt("dve", u2, u2, u1, ALU.add, [self.rrt, self.rrt2], [self.rrt2])
        self.tt("dve", outi, u2, gi, ALU.add, [self.rrt2, rG], [rX])

    def recur(self):
        S = self.S
        XP, G = self.XP, self.Gall
        S.op("dve", lambda e: e.tensor_copy(out=XP[:, :, :, 0], in_=self.Xin), reads=[self.rX], writes=[self.rX])
        for k in range(63):
            self.cstep(XP[:, 0, :, k + 1], XP[:, 1, :, k + 1], XP[:, 0, :, k], XP[:, 1, :, k], G[:, 0, :, k], G[:, 1, :, k])

    def stub_zero(self, dram):
        S = self.S
        A = self.A
        A.off = self.base_off
        z = A.take([4, 1024], BF16)
        rz = R()
        S.op("dve", lambda e: e.memset(z, 0.0), writes=[rz])
        dv = dram.rearrange("(c p) t -> p c t", p=128)
        for i in range(4):
            S.dma("sp", dv[:, :, 1024 * i:1024 * i + 1024], z, reads=[rz], writes=[R()])
        S.barrier(self.tok_s)

    def phase4(self):
        S = self.S
        self.alloc_tile_bufs(self.base_noU)
        A = self.A
        h, rh, xn, rxn = self.h, self.rh, self.xn, self.rxn
        at = A.take([4, 1024], BF16)
        ys = A.take([4, 1024], BF16)
        pt = A.take([2, 1024], BF16)
        wao = A.take([4, 1024], BF16)
        wso = A.take([4, 1024], BF16)
        wpp = A.take([2, 1024], BF16)
        t1 = A.take([512], F32)
        t2 = A.take([512], F32)
        rat, rys, rpt, rwr, rt1, rt2 = R(), R(), R(), R(), R(), R()
        save = A.off
        A.off = self.big_off
        mT = A.take([8, 1024], BF16)
        A.off = self.big_off
        outf = A.take([8, 1024], F32)
        A.off = save
        rbig = self.raT
        S.dma("pool", wao, self.w_ao.rearrange("p (k m) -> p k m", m=1024), writes=[rwr])
        S.dma("pool", wso, self.w_so.rearrange("p (k m) -> p k m", m=1024), writes=[rwr])
        S.dma("pool", wpp, self.w_pp.rearrange("p (k m) -> p k m", m=1024), writes=[rwr])
        h1v = self.h1_s.rearrange("(kc p) t -> p kc t", p=128)
        av = self.attnT_s.rearrange("(c p) t -> p c t", p=128)
        yv = self.yssmT_s.rearrange("(c p) t -> p c t", p=128)
        pv = self.pT.rearrange("(c p) t -> p c t", p=128)
        ov = self.outT.rearrange("(kc p) t -> p kc t", p=128)
        subs = [(0, 512), (512, 512)]
        for ti in range(4):
            tk = slice(1024 * ti, 1024 * ti + 1024)
            S.dma("sp", h, h1v[:, :, tk], writes=[rh])
            S.dma("sp", at, av[:, :, tk], writes=[rat])
            S.dma("sp", ys, yv[:, :, tk], writes=[rys])
            S.dma("pool", pt, pv[:, :, tk], writes=[rpt])
            self.rmsnorm(h, rh, 1, xn, rxn, subs)
            for half in range(2):
                wga, rwga = self.load_w(self.w_in[4 + half], 8, 512)
                wgs, rwgs = self.load_w(self.w_in[6 + half], 8, 512)
                for (t0, n) in subs:
                    for cj in range(4):
                        oc = half * 4 + cj
                        for (wg, rwg, wbr, src, rsrc, first) in ((wga, rwga, wao, at, rat, True),
                                                                 (wgs, rwgs, wso, ys, rys, False)):
                            pg, rpg = S.bank()
                            for kc in range(8):
                                S.op("pe", lambda e, kc=kc, cj=cj, t0=t0, n=n, pg=pg, wg=wg: e.matmul(
                                    pg[:, 0:n], wg[:, kc, cj * 128:(cj + 1) * 128], xn[:, kc, t0:t0 + n],
                                    start=(kc == 0), stop=(kc == 7)), reads=[rwg, rxn], writes=[rpg], track=(kc == 7))
                            pb, rpb = S.bank()
                            for kc in range(4):
                                S.op("pe", lambda e, kc=kc, oc=oc, t0=t0, n=n, pb=pb, wbr=wbr, src=src: e.matmul(
                                    pb[:, 0:n], wbr[:, kc, oc * 128:(oc + 1) * 128], src[:, kc, t0:t0 + n],
                                    start=(kc == 0), stop=(kc == 3)), reads=[rwr, rsrc], writes=[rpb], track=(kc == 3))
                            tmp, rtmp = self.tmpf[self.tmpn % 2]
                            self.tmpn += 1
                            S.op("act", lambda e, n=n, pg=pg, tmp=tmp: e.activation(out=tmp[:, 0:n], in_=pg[:, 0:n], func=AF.Sigmoid),
                                 reads=[rpg], writes=[rtmp])
                            if first:
                                S.op("dve", lambda e, n=n, pb=pb, tmp=tmp: e.tensor_tensor(
                                    out=t1[:, 0:n], in0=tmp[:, 0:n], in1=pb[:, 0:n], op=ALU.mult),
                                     reads=[rtmp, rpb], writes=[rt1])
                            else:
                                S.op("dve", lambda e, n=n, pb=pb, tmp=tmp: e.tensor_tensor(
                                    out=t2[:, 0:n], in0=tmp[:, 0:n], in1=pb[:, 0:n], op=ALU.mult),
                                     reads=[rtmp, rpb], writes=[rt2])
                                S.op("dve", lambda e, n=n, t0=t0, oc=oc: e.tensor_tensor(
                                    out=mT[:, oc, t0:t0 + n], in0=t1[:, 0:n], in1=t2[:, 0:n], op=ALU.add),
                                     reads=[rt1, rt2], writes=[rbig])
            for mb in range(2):
                w, rw = self.load_w(self.w_out[mb], 8, 512)
                for (t0, n) in subs:
                    for cj in range(4):
                        oc = mb * 4 + cj
                        ps, pr = S.bank()
                        for kc in range(8):
                            S.op("pe", lambda e, kc=kc, cj=cj, t0=t0, n=n, ps=ps, w=w: e.matmul(
                                ps[:, 0:n], w[:, kc, cj * 128:(cj + 1) * 128], mT[:, kc, t0:t0 + n],
                                start=(kc == 0), stop=(kc == 7)), reads=[rw, rbig], writes=[pr], track=(kc == 7))
                        S.op("dve", lambda e, oc=oc, t0=t0, n=n, ps=ps: e.tensor_tensor(
                            out=h[:, oc, t0:t0 + n], in0=ps[:, 0:n], in1=h[:, oc, t0:t0 + n], op=ALU.add),
                             reads=[pr, rh], writes=[rh])
            self.rmsnorm(h, rh, 2, xn, rxn, subs)
            self.ffn(self.w_gu2, self.w_d2, xn, rxn, h, rh, subs)
            self.rmsnorm(h, rh, 3, xn, rxn, subs)
            for mb in range(2):
                w, rw = self.load_w(self.w_pg[mb], 8, 512)
                for (t0, n) in subs:
                    for cj in range(4):
                        oc = mb * 4 + cj
                        pg, rpg = S.bank()
                        for kc in range(8):
                            S.op("pe", lambda e, kc=kc, cj=cj, t0=t0, n=n, pg=pg, w=w: e.matmul(
                                pg[:, 0:n], w[:, kc, cj * 128:(cj + 1) * 128], xn[:, kc, t0:t0 + n],
                                start=(kc == 0), stop=(kc == 7)), reads=[rw, rxn], writes=[rpg], track=(kc == 7))
                        pp, rpp = S.bank()
                        for kc in range(2):
                            S.op("pe", lambda e, kc=kc, oc=oc, t0=t0, n=n, pp=pp: e.matmul(
                                pp[:, 0:n], wpp[:, kc, oc * 128:(oc + 1) * 128], pt[:, kc, t0:t0 + n],
                                start=(kc == 0), stop=(kc == 1)), reads=[rwr, rpt], writes=[rpp], track=(kc == 1))
                        tmp, rtmp = self.tmpf[self.tmpn % 2]
                        self.tmpn += 1
                        S.op("act", lambda e, n=n, pg=pg, tmp=tmp: e.activation(out=tmp[:, 0:n], in_=pg[:, 0:n], func=AF.Sigmoid),
                             reads=[rpg], writes=[rtmp])
                        S.op("dve", lambda e, n=n, pp=pp, tmp=tmp: e.tensor_tensor(
                            out=t1[:, 0:n], in0=tmp[:, 0:n], in1=pp[:, 0:n], op=ALU.mult),
                             reads=[rtmp, rpp], writes=[rt1])
                        S.op("dve", lambda e, oc=oc, t0=t0, n=n: e.tensor_tensor(
                            out=h[:, oc, t0:t0 + n], in0=t1[:, 0:n], in1=h[:, oc, t0:t0 + n], op=ALU.add),
                             reads=[rt1, rh], writes=[rh])
            self.rmsnorm(h, rh, 4, outf, rbig, subs)
            S.dma("sp", ov[:, :, tk], outf, reads=[rbig], writes=[R()])

    def build(self):
        self.consts()
        if not DBG.get("skip_p1"):
            self.phase1()
        if STUB_ATTN:
            self.stub_zero(self.attnT_s)
        else:
            self.phase2()
        if STUB_SSM:
            self.stub_zero(self.yssmT_s)
        else:
            self.phase3a()
            if DBG.get("p3_stop", 99) >= 5:
                self.phase3b()
        if DBG.get("dump"):
            self.S.dma("sp", self.dbg_attn[:, :], self.attnT_s[:, :], writes=[R()])
            self.S.dma("sp", self.dbg_ssm[:, :], self.yssmT_s[:, :], writes=[R()])
        if not DBG.get("skip_p4"):
            self.phase4()
        self.S.emit()
        return self.nc


def _blk(W, bw):
    K, M = W.shape
    KC = K // 128
    a = W.reshape(KC, 128, M // bw, bw).transpose(2, 1, 0, 3)
    return np.ascontiguousarray(a).reshape(M // bw, 128, KC * bw)


def _gu(Wg, Wu):
    g = Wg.reshape(8, 128, 11, 256).transpose(2, 1, 0, 3)
    u = Wu.reshape(8, 128, 11, 256).transpose(2, 1, 0, 3)
    a = np.concatenate([g, u], axis=3)
    return np.ascontiguousarray(a).reshape(11, 128, 4096)


def _maskcols(half):
    mc = np.zeros((128, 45), np.float32)
    mc[0:64, 1] = MASKV
    mc[64:128, 2] = MASKV
    base = 64 * half - 4
    for si, l in enumerate([4, 5, 6, 7, 65, 66, 67]):
        r = base + l
        rs = min(max(r - 4, 0), 120)
        lo = rs - base
        pi0 = 0 if l < 8 else 30
        for j in range(6):
            for k2 in range(2):
                row = 2 * (pi0 + j) + k2
                if not (lo <= row <= lo + 7):
                    mc[64 * k2:64 * k2 + 64, 3 + si * 6 + j] = MASKV
    return mc


_NC_CACHE = {}


def kernel(**inp):
    f = lambda a: np.asarray(a, dtype=np.float32)
    x = f(inp["x"])
    p = f(inp["p"])[0]
    shared = {
        "w_gu1": _gu(f(inp["ffn1_w_gate"])[0], f(inp["ffn1_w_up"])[0]),
        "w_d1": _blk(f(inp["ffn1_w_down"])[0], 128),
        "w_gu2": _gu(f(inp["ffn2_w_gate"])[0], f(inp["ffn2_w_up"])[0]),
        "w_d2": _blk(f(inp["ffn2_w_down"])[0], 128),
        "w_in": _blk(f(inp["w_in"])[0], 512),
        "w_out": _blk(f(inp["w_out"])[0], 512),
        "w_pg": _blk(f(inp["ple_w_gate"])[0], 512),
        "w_ao": _blk(f(inp["w_attn_out"])[0], 1024)[0],
        "w_so": _blk(f(inp["w_ssm_out"])[0], 1024)[0],
        "w_pp": _blk(f(inp["ple_w_proj"])[0], 1024)[0],
        "w_glu": _blk(f(inp["ssm_glu_w"])[0], 512)[0],
        "glu_b": np.ascontiguousarray(f(inp["ssm_glu_b"])[0].reshape(4, 128).T),
        "cst_ident": np.eye(128, dtype=np.float32),
        "rpb": np.ascontiguousarray(f(inp["na_rpb"])[0]),
        "ssm_dd": np.ascontiguousarray(f(inp["ssm_d"])[0].reshape(1, 512)),
        "cst_expo": np.ascontiguousarray(np.broadcast_to(np.arange(65, dtype=np.float32)[None], (64, 65))),
        "cst_eye16": np.eye(16, dtype=np.float32).reshape(1, 256),
    }
    lr_ = f(inp["ssm_lam_re"])[0].reshape(64, 64).T
    li_ = f(inp["ssm_lam_im"])[0].reshape(64, 64).T
    ld_ = np.broadcast_to(f(inp["ssm_log_dt"])[0].reshape(1, 64), (64, 64))
    shared["ssm_p"] = np.ascontiguousarray(np.stack([lr_, li_, ld_], axis=1))
    shared["ssm_b"] = np.ascontiguousarray(np.stack(
        [f(inp[k])[0].reshape(64, 64, 16).transpose(1, 0, 2) for k in ("ssm_b_re", "ssm_b_im")], axis=1))
    shared["ssm_c"] = np.ascontiguousarray(np.stack(
        [f(inp[k])[0].reshape(64, 16, 64).transpose(2, 0, 1) for k in ("ssm_c_re", "ssm_c_im")], axis=1))
    gl = [f(inp["ffn1_norm"])[0], f(inp["mix_norm"])[0], f(inp["ffn2_norm"])[0], f(inp["ple_norm"])[0],
          f(inp["final_norm"]), f(inp["final_norm"])]
    shared["gains"] = np.ascontiguousarray(
        np.stack([g.reshape(8, 128).T for g in gl], axis=1).reshape(128, 48))
    in_maps = []
    for c in range(NCORES):
        b, half = c // 2, c % 2
        lo = 4096 * half - HALO
        xl = np.zeros((NLOC, D), np.float32)
        s0, s1 = max(lo, 0), min(lo + NLOC, 8192)
        xl[s0 - lo:s1 - lo] = x[b, s0:s1]
        m = dict(shared)
        m["xT"] = np.ascontiguousarray(xl.T)
        m["maskcols"] = _maskcols(half)
        sl = np.zeros((64, 16), np.float32)
        if half == 1:
            sl[:, c - 1] = 1.0
        else:
            sl[:, 8 + c + 1] = 1.0
        m["sel"] = sl
        m["pT"] = np.ascontiguousarray(p[b, 4096 * half:4096 * half + 4096].T)
        in_maps.append(m)
    if "nc" not in _NC_CACHE:
        _NC_CACHE["nc"] = Builder().build()
    res = run_bass_kernel_spmd(_NC_CACHE["nc"], in_maps, core_ids=list(range(NCORES)))
    if DBG.get("dump"):
        DBG["res"] = res
    out = np.empty((4, 8192, D), np.float32)
    for c in range(NCORES):
        b, half = c // 2, c % 2
        out[b, 4096 * half:4096 * half + 4096] = res.results[c]["outT"].T
    return out
```
